# Optimizing a Trainium2 kernel written in Bass

```python
import jax
import jax.numpy as jnp
from jax import lax

D_MODEL = 2048
BATCH = 1
SEQ = 16384
DEPTH = 2

GRID_W = 64
CTX_LEN = 256
D_HGRN = 1024
HGRN_HEADS = 8
HGRN_HEAD_DIM = D_HGRN // HGRN_HEADS
CHUNK = 64
D_POOL = 512
POOL_WINDOWS = (2, 4, 8, 16)
POOL_GROUP = D_POOL // len(POOL_WINDOWS)
D_CONV = 512
D_FF = 5632
D_IN = 5 * D_HGRN + D_POOL + 3 * D_CONV
ALPHA = (2 * DEPTH) ** 0.25
BETA = (8 * DEPTH) ** -0.25
EPS = 1e-6
F_MIN = 1e-30
F32 = jnp.float32

kernel_name = 'hybrid_hgrn2_pool_shortconv_dit'


def layer_norm(x, g=None, b=None):
    xf = x.astype(F32)
    mu = jnp.mean(xf, -1, keepdims=True)
    var = jnp.mean(jnp.square(xf - mu), -1, keepdims=True)
    y = (xf - mu) * lax.rsqrt(var + EPS)
    if g is not None:
        y = y * g.astype(F32) + b.astype(F32)
    return y.astype(x.dtype)


def modulate(x, shift, scale):
    return layer_norm(x) * (1 + scale) + shift


def dwconv3(x, w, b=None):
    y = lax.conv_general_dilated(x, w[:, None, :].astype(x.dtype), (1,), ((1, 1),),
                                 dimension_numbers=('NWC', 'WIO', 'NWC'),
                                 feature_group_count=x.shape[-1])
    return y if b is None else y + b


def chunk_gla(q, k, v, log_f, s0):
    B, T, H, dk = q.shape
    n = T // CHUNK
    def to_chunks(a):
        return a.astype(F32).reshape(B, n, CHUNK, H, a.shape[-1]).transpose(1, 0, 3, 2, 4)
    qc, kc, vc, gc = (to_chunks(a) for a in (q, k, v, log_f))
    causal = jnp.tril(jnp.ones((CHUNK, CHUNK), bool))[:, :, None]
    def step(S, inp):
        qi, ki, vi, gi = inp
        bcum = jnp.cumsum(gi, axis=-2)
        diff = bcum[..., :, None, :] - bcum[..., None, :, :]
        decay = jnp.where(causal, jnp.exp(jnp.where(causal, diff, 0.0)), 0.0)
        attn = jnp.einsum('bhtsk,bhsk->bhts', qi[..., :, None, :] * decay, ki)
        o = (jnp.einsum('bhts,bhsv->bhtv', attn, vi)
             + jnp.einsum('bhtk,bhkv->bhtv', qi * jnp.exp(bcum), S))
        b_last = bcum[..., -1:, :]
        S_new = (jnp.exp(b_last[..., 0, :])[..., None] * S
                 + jnp.einsum('bhsk,bhsv->bhkv', ki * jnp.exp(b_last - bcum), vi))
        return S_new, o
    S, o = lax.scan(step, s0, (qc, kc, vc, gc))
    o = o.transpose(1, 0, 3, 2, 4).reshape(B, T, H, v.shape[-1])
    return o, S


def forget_gate(z, lb):
    z = z.astype(F32)
    f = lb + (1.0 - lb) * jax.nn.sigmoid(z)
    log_f = jnp.log(jnp.maximum(f, F_MIN))
    k = (1.0 - lb) * jax.nn.sigmoid(-z)
    return log_f, k


def hgrn2_bidir(p, lb_f, lb_b, norm_w, s0_f, s0_b, with_output):
    B, T, _ = p.shape
    def heads(a):
        return a.reshape(B, T, HGRN_HEADS, HGRN_HEAD_DIM)
    q = heads(jax.nn.silu(p[..., :D_HGRN].astype(F32)))
    i = heads(p[..., D_HGRN:2 * D_HGRN])
    lf_f, k_f = forget_gate(p[..., 2 * D_HGRN:3 * D_HGRN], lb_f)
    lf_b, k_b = forget_gate(p[..., 3 * D_HGRN:4 * D_HGRN], lb_b)
    lf_f, k_f, lf_b, k_b = heads(lf_f), heads(k_f), heads(lf_b), heads(k_b)
    def flip(a):
        return jnp.flip(a, 1)
    o_f, s_f = chunk_gla(q, k_f, i, lf_f, s0_f)
    o_b, s_b = chunk_gla(flip(q), flip(k_b), flip(i), flip(lf_b), s0_b)
    if not with_output:
        return None, s_f, s_b
    o = o_f + flip(o_b)
    o = o * lax.rsqrt(jnp.mean(jnp.square(o), -1, keepdims=True) + EPS)
    o = o * norm_w.astype(F32).reshape(HGRN_HEADS, HGRN_HEAD_DIM)
    g = p[..., 4 * D_HGRN:5 * D_HGRN].astype(F32)
    o = o.reshape(B, T, D_HGRN) * jax.nn.silu(g)
    return o.astype(p.dtype), s_f, s_b


def pool_mixer(v, pool_w, pool_scale):
    N, T, _ = v.shape
    vf = v.astype(F32)
    cs = jnp.pad(jnp.cumsum(vf, 1), ((0, 0), (1, 0), (0, 0)))
    t = jnp.arange(T)
    means = []
    for gi, w in enumerate(POOL_WINDOWS):
        lo = jnp.maximum(t - w // 2, 0)
        hi = jnp.minimum(t + w // 2, T)
        csg = cs[..., gi * POOL_GROUP:(gi + 1) * POOL_GROUP]
        cnt = (hi - lo).astype(F32)[:, None]
        means.append((jnp.take(csg, hi, 1) - jnp.take(csg, lo, 1)) / cnt)
    d = jnp.stack(means, 2) - vf.reshape(N, T, len(POOL_WINDOWS), POOL_GROUP)
    y = jnp.einsum('ntgp,gpq->ntgq', d, pool_w.astype(F32)).reshape(N, T, D_POOL)
    return (y * pool_scale.astype(F32)).astype(v.dtype)


def local_mixers(p_loc, pool_w, pool_scale, conv_w):
    v = p_loc[..., :D_POOL]
    bg, cg, h = jnp.split(p_loc[..., D_POOL:], 3, -1)
    y_conv = bg * dwconv3(cg * h, conv_w)
    return jnp.concatenate([pool_mixer(v, pool_w, pool_scale), y_conv], -1)


def conv_ffn(h, w_up, conv_w, conv_b, w_down):
    u, g = jnp.split(h @ w_up, 2, -1)
    return (u * jax.nn.silu(dwconv3(g, conv_w, conv_b))) @ w_down


def setup_inputs(seed: int = 0):
    key = jax.random.key(seed)
    ks = iter(jax.random.split(key, 24))
    def nrm(shape, s):
        return s * jax.random.normal(next(ks), shape, F32)
    D = D_MODEL
    return {
        'x': nrm((BATCH, SEQ, D), 1.0),
        'c': nrm((BATCH, D), 1.0),
        'ctx': nrm((BATCH, CTX_LEN, D), 1.0),
        'c_ctx': nrm((D,), 1.0),
        'w_ada': nrm((DEPTH, D, 6 * D), 0.5 * D ** -0.5),
        'b_ada': nrm((DEPTH, 6 * D), 0.01),
        'w_in': nrm((DEPTH, D, D_IN), D ** -0.5),
        'lb_logits': nrm((2, DEPTH, D_HGRN), 0.5),
        'hgrn_norm_w': 1.0 + nrm((DEPTH, D_HGRN), 0.02),
        'pool_w': nrm((DEPTH, len(POOL_WINDOWS), POOL_GROUP, POOL_GROUP), POOL_GROUP ** -0.5),
        'pool_scale': 1.0 + nrm((DEPTH, D_POOL), 0.1),
        'conv_w': nrm((DEPTH, 3, D_CONV), 3 ** -0.5),
        'w_out': nrm((DEPTH, D, D), BETA * D ** -0.5),
        'ln1_g': 1.0 + nrm((DEPTH, D), 0.02),
        'ln1_b': nrm((DEPTH, D), 0.02),
        'w_up': nrm((DEPTH, D, 2 * D_FF), D ** -0.5),
        'ffn_conv_w': nrm((DEPTH, 3, D_FF), 3 ** -0.5),
        'ffn_conv_b': nrm((DEPTH, D_FF), 0.01),
        'w_down': nrm((DEPTH, D_FF, D), BETA * D_FF ** -0.5),
        'ln2_g': 1.0 + nrm((DEPTH, D), 0.02),
        'ln2_b': nrm((DEPTH, D), 0.02),
    }


def reference(x, c, ctx, c_ctx, w_ada, b_ada, w_in, lb_logits, hgrn_norm_w, pool_w,
              pool_scale, conv_w, w_out, ln1_g, ln1_b, w_up, ffn_conv_w, ffn_conv_b,
              w_down, ln2_g, ln2_b):
    B, L, _ = x.shape
    rows = L // GRID_W
    def to_rows(a):
        return a.reshape(B * rows, GRID_W, a.shape[-1])
    def from_rows(a):
        return a.reshape(B, L, a.shape[-1])
    p_lb = jax.nn.softmax(lb_logits.astype(F32), axis=1)
    lower = jnp.cumsum(p_lb, 1) - p_lb[:, :1]
    zero_state = jnp.zeros((B, HGRN_HEADS, HGRN_HEAD_DIM, HGRN_HEAD_DIM), F32)
    for l in range(DEPTH):
        ctx_live = l < DEPTH - 1
        mod_x = jnp.split(jax.nn.silu(c) @ w_ada[l] + b_ada[l], 6, -1)
        mod_c = jnp.split(jax.nn.silu(c_ctx) @ w_ada[l] + b_ada[l], 6, -1)
        sh1, sc1, g1, sh2, sc2, g2 = (m[:, None, :] for m in mod_x)
        csh1, csc1, cg1, csh2, csc2, cg2 = mod_c
        hx = modulate(x, sh1, sc1)
        hc = modulate(ctx, csh1, csc1)
        px = hx @ w_in[l]
        pc = hc @ w_in[l, :, :(D_IN if ctx_live else 4 * D_HGRN)]
        oc, s_f, s_b = hgrn2_bidir(pc, lower[0, l], lower[1, l], hgrn_norm_w[l],
                                   zero_state, zero_state, ctx_live)
        ox, _, _ = hgrn2_bidir(px, lower[0, l], lower[1, l], hgrn_norm_w[l], s_f, s_b, True)
        loc_x = from_rows(local_mixers(to_rows(px[..., 5 * D_HGRN:]), pool_w[l],
                                       pool_scale[l], conv_w[l]))
        yx = jnp.concatenate([ox, loc_x], -1) @ w_out[l]
        x = layer_norm(ALPHA * x + g1 * yx, ln1_g[l], ln1_b[l])
        if ctx_live:
            loc_c = local_mixers(pc[..., 5 * D_HGRN:], pool_w[l], pool_scale[l], conv_w[l])
            yc = jnp.concatenate([oc, loc_c], -1) @ w_out[l]
            ctx = layer_norm(ALPHA * ctx + cg1 * yc, ln1_g[l], ln1_b[l])
        hx = modulate(x, sh2, sc2)
        fx = from_rows(conv_ffn(to_rows(hx), w_up[l], ffn_conv_w[l], ffn_conv_b[l], w_down[l]))
        x = layer_norm(ALPHA * x + g2 * fx, ln2_g[l], ln2_b[l])
        if ctx_live:
            hc = modulate(ctx, csh2, csc2)
            fc = conv_ffn(hc, w_up[l], ffn_conv_w[l], ffn_conv_b[l], w_down[l])
            ctx = layer_norm(ALPHA * ctx + cg2 * fc, ln2_g[l], ln2_b[l])
    return x
```

```python
import contextlib
import itertools
import os
import numpy as np
import ml_dtypes
import concourse.bass as bass
import concourse.mybir as mybir
from concourse.bass_utils import run_bass_kernel_spmd

F32 = mybir.dt.float32
BF16 = mybir.dt.bfloat16
AF = mybir.ActivationFunctionType
ALU = mybir.AluOpType
AX = mybir.AxisListType

D = 2048
KC = D // 128
DEPTH = 2
D_HGRN = 1024
NH = 8
D_IN = 7168
D_FF = 5632
NJ = D_FF // 128
ALPHA = (2 * DEPTH) ** 0.25
EPS = 1e-6
F_MIN = 1e-30
NCORES = 8
PAD = 8
RW = 64 + 2 * PAD

BIGW = {"w_ada": (D, 6 * D), "w_in": (D, D_IN), "w_out": (D, D), "w_up": (D, 2 * D_FF), "w_down": (D_FF, D)}
SMALLW = {"b_ada": (1, 6 * D), "hgrn_norm_w": (1, D_HGRN), "pool_w": (512, 128), "pool_scale": (1, 512),
          "conv_w": (3, 512), "ln1_g": (1, D), "ln1_b": (1, D), "ffn_conv_w": (3, D_FF),
          "ffn_conv_b": (1, D_FF), "ln2_g": (1, D), "ln2_b": (1, D)}


class Buf:
    __slots__ = ("w", "r", "psum")

    def __init__(self):
        self.w = None
        self.r = []
        self.psum = False


class Phase:
    ENGS = ("pe", "act", "dve", "pool", "sp")
    NDS = 8
    G = None

    @classmethod
    def reset_globals(cls, nc):
        sems = {}
        for e in cls.ENGS:
            sems[("e", e)] = nc.alloc_semaphore(name=f"g_{e}")
        for de in ("sp", "pool"):
            for k in range(cls.NDS):
                sems[("d", (de, k))] = nc.alloc_semaphore(name=f"g_d{de}{k}")
        cls.G = {"sems": sems, "ecnt": {e: 0 for e in cls.ENGS}, "dcnt": {"sp": 0, "pool": 0}, "dlast": {}}

    def __init__(self, nc, name):
        self.nc = nc
        self.name = name
        self.ops = {e: [] for e in self.ENGS}
        self.cnt = {e: 0 for e in self.ENGS}
        self.dma_toks = {}

    def _deps(self, reads, writes):
        d = []
        for b in reads:
            if b.w is not None:
                d.append(b.w)
        for b in writes:
            if b.w is not None:
                d.append(b.w)
            d.extend(b.r)
        return d

    def _commit(self, tok, reads, writes):
        for b in reads:
            b.r.append(tok)
        for b in writes:
            b.w = tok
            b.r = []

    def op(self, eng, meth, reads=(), writes=(), **kw):
        if meth in ("matmul", "transpose"):
            for b in writes:
                b.psum = True
        writes = list(writes) + [b for b in reads if b.psum]
        reads = [b for b in reads if not b.psum]
        deps = self._deps(reads, writes)
        self.cnt[eng] += 1
        tok = ("e", eng, self.cnt[eng])
        self.ops[eng].append((meth, kw, deps, tok))
        self._commit(tok, reads, writes)
        return tok

    def dma(self, eng, out, in_, reads=(), writes=()):
        deps = self._deps(reads, writes)
        lst = self.dma_toks.setdefault(eng, [])
        n = self.G["dcnt"][eng] + len(lst)
        k = n % self.NDS
        val = 16 * (n // self.NDS + 1)
        if len(lst) >= self.NDS:
            deps.append(lst[len(lst) - self.NDS])
        tok = ("d", (eng, k), val)
        lst.append(tok)
        self.ops[eng].append(("dma_start", dict(out=out, in_=in_), deps, tok))
        self._commit(tok, reads, writes)
        return tok

    def mm(self, out, lhsT, rhs, start, stop, reads, writes):
        return self.op("pe", "matmul", reads, writes, out=out, lhsT=lhsT, rhs=rhs, start=start, stop=stop)

    def tr(self, out, in_, ident, reads, writes):
        return self.op("pe", "transpose", reads, writes, out=out, in_=in_, identity=ident)

    def act(self, out, in_, func, reads, writes, bias=None, scale=None):
        kw = dict(out=out, in_=in_, func=func)
        if bias is not None:
            kw["bias"] = bias
        if scale is not None:
            kw["scale"] = scale
        return self.op("act", "activation", reads, writes, **kw)

    def ts(self, eng, out, in0, s1, s2, op0, op1, reads, writes):
        kw = dict(out=out, in0=in0, scalar1=s1, scalar2=s2, op0=op0)
        if op1 is not None:
            kw["op1"] = op1
        return self.op(eng, "tensor_scalar", reads, writes, **kw)

    def tt(self, eng, out, in0, in1, op, reads, writes):
        return self.op(eng, "tensor_tensor", reads, writes, out=out, in0=in0, in1=in1, op=op)

    def stt(self, out, in0, scalar, in1, op0, op1, reads, writes):
        return self.op("dve", "scalar_tensor_tensor", reads, writes, out=out, in0=in0, scalar=scalar, in1=in1,
                       op0=op0, op1=op1)

    def copy(self, eng, out, in_, reads, writes):
        if eng == "act":
            return self.op("act", "copy", reads, writes, out=out, in_=in_)
        return self.op(eng, "tensor_copy", reads, writes, out=out, in_=in_)

    def memset(self, eng, ap, val, writes):
        return self.op(eng, "memset", (), writes, ap=ap, constant=val)

    def run(self):
        nc = self.nc
        G = self.G
        sems = G["sems"]
        refd = set()
        for e in self.ENGS:
            for meth, kw, deps, tok in self.ops[e]:
                for d_ in deps:
                    if d_[0] == "e" and not (d_[1] == e and e in ("pe", "sp")):
                        refd.add((d_[1], d_[2]))
        val = {}
        for e in self.ENGS:
            c = G["ecnt"][e]
            for i in range(1, self.cnt[e] + 1):
                if (e, i) in refd or i == self.cnt[e]:
                    c += 1
                val[(e, i)] = c + (0 if ((e, i) in refd or i == self.cnt[e]) else 1)
        endv = {}
        for e in self.ENGS:
            c = G["ecnt"][e]
            for i in range(1, self.cnt[e] + 1):
                if (e, i) in refd or i == self.cnt[e]:
                    c += 1
            endv[e] = c
        with nc.Block() as block:
            def emit(eng_name, eng):
                waited = {}
                for meth, kw, deps, tok in self.ops[eng_name]:
                    need = {}
                    for dt_ in deps:
                        key = (dt_[0], dt_[1])
                        if key == ("e", eng_name) and eng_name in ("pe", "sp"):
                            continue
                        v = val[(dt_[1], dt_[2])] if dt_[0] == "e" else dt_[2]
                        if v > need.get(key, 0):
                            need[key] = v
                    for key, v in need.items():
                        if waited.get(key, 0) >= v:
                            continue
                        eng.wait_ge(sems[key], v)
                        waited[key] = v
                    ins = getattr(eng, meth)(**kw)
                    if tok[0] == "e":
                        if (tok[1], tok[2]) in refd or tok[2] == self.cnt[eng_name]:
                            ins.then_inc(sems[("e", eng_name)], 1)
                    else:
                        ins.then_inc(sems[("d", tok[1])], 16)
                        G["dlast"][tok[1]] = max(G["dlast"].get(tok[1], 0), tok[2])
                if eng_name == "sp":
                    for de in ("sp", "pool"):
                        for tk in self.dma_toks.get(de, []):
                            G["dlast"][tk[1]] = max(G["dlast"].get(tk[1], 0), tk[2])
                    for k, v in G["dlast"].items():
                        eng.wait_ge(sems[("d", k)], v)
                    for e2 in ("pe", "act", "dve", "pool"):
                        if self.cnt[e2]:
                            eng.wait_ge(sems[("e", e2)], endv[e2])

            @block.tensor
            def _(e):
                emit("pe", e)

            @block.scalar
            def _(e):
                emit("act", e)

            @block.vector
            def _(e):
                emit("dve", e)

            @block.gpsimd
            def _(e):
                emit("pool", e)

            @block.sync
            def _(e):
                emit("sp", e)
        for e in self.ENGS:
            G["ecnt"][e] = endv[e]
        for de in ("sp", "pool"):
            G["dcnt"][de] += len(self.dma_toks.get(de, []))


class Cfg:
    def __init__(self, nx=2048, nctx=256, dbg=None, wshard=False, layers=(0, 1),
                 need=("w_ada", "w_in", "w_out", "w_up", "w_down"),
                 phases=("mod", "ln1", "H", "L", "X", "C", "O", "U", "D")):
        self.NX = nx
        self.NCTX = nctx
        self.T = nx + nctx
        self.NT = self.T // 128
        self.NTC = nctx // 128
        self.dbg = dbg or ()
        self.wshard = wshard
        self.layers = tuple(layers)
        self.need = tuple(need)
        self.phases = tuple(phases)


def token_groups(cfg, live_ctx=True, gsz=512):
    out = []
    if live_ctx:
        t = 0
        while t < cfg.NCTX:
            n = min(gsz, cfg.NCTX - t)
            out.append((t, n, True))
            t += n
    t = cfg.NCTX
    while t < cfg.T:
        n = min(gsz, cfg.T - t)
        out.append((t, n, False))
        t += n
    return out


class Ctx:
    pass


_uid = itertools.count()


def _n(s):
    return f"{s}_{next(_uid)}"


def build_program(cfg):
    nc = bass.Bass("TRN2", target_bir_lowering=False)
    T, NT, NX, NCTX = cfg.T, cfg.NT, cfg.NX, cfg.NCTX
    g = Ctx()
    g.nc, g.cfg = nc, cfg
    Phase.reset_globals(nc)

    def din(name, shape, dt=F32):
        return nc.dram_tensor(name, list(shape), dt, kind="ExternalInput").ap()

    I = {}
    g.I = I
    I["x"] = din("x", [NX, D])
    I["ctx"] = din("ctx", [NCTX, D])
    I["cc"] = din("cc", [2, D])
    I["lb_logits"] = din("lb_logits", [4, D_HGRN])
    g.W = {}
    for l in cfg.layers:
        for nm, (r, c) in BIGW.items():
            if nm in cfg.need:
                if cfg.wshard:
                    I[f"{nm}{l}"] = din(f"{nm}{l}", [r // NCORES, c])
                    g.W[(nm, l)] = nc.dram_tensor(f"W_{nm}{l}", [r, c], F32).ap()
                else:
                    I[f"{nm}{l}"] = din(f"{nm}{l}", [r, c])
                    g.W[(nm, l)] = I[f"{nm}{l}"]
        for nm, (r, c) in SMALLW.items():
            I[f"{nm}{l}"] = din(f"{nm}{l}", [r, c])
            g.W[(nm, l)] = I[f"{nm}{l}"]
    I["ident_bf"] = din("ident_bf", [128, 128], BF16)
    I["ident_f"] = din("ident_f", [128, 128])
    I["maskf"] = din("maskf", [128, 128])
    I["maskb"] = din("maskb", [128, 128])
    I["rmask"] = din("rmask", [1, T], BF16)
    I["coresel"] = din("coresel", [128, 2, NCORES])

    out = nc.dram_tensor("out", [NX, D], F32, kind="ExternalOutput").ap()
    g.dbg = {}
    for nm, shp, dt in cfg.dbg:
        g.dbg[nm] = nc.dram_tensor("dbg_" + nm, list(shp), dt, kind="ExternalOutput").ap()

    g.xres = nc.dram_tensor("xres", [T, D], F32).ap()
    g.mod_d = nc.dram_tensor("mod_d", [2, 6 * D], F32).ap()
    g.sg_d = nc.dram_tensor("sg_d", [128, NH, T], BF16).ap()
    g.qp_d = nc.dram_tensor("qp_d", [128, 2 * NH, NX], BF16).ap()
    g.oloc_d = nc.dram_tensor("oloc_d", [T, NH, 128], F32).ap()
    g.catl_d = nc.dram_tensor("catl_d", [128, 8, T], BF16).ap()
    g.XW = 2 * NH * 128 + 2 * NH
    g.xch_src = nc.dram_tensor("xch_src", [128, g.XW], F32).ap()
    g.xch_dst = nc.dram_tensor("xch_dst", [NCORES * 128, g.XW], F32).ap()
    g.sctx_d = nc.dram_tensor("sctx_d", [128, 2, NH, 128], F32).ap()
    g.y_d = nc.dram_tensor("y_d", [T, D], F32).ap()
    g.hid_d = nc.dram_tensor("hid_d", [NJ, 128, T], BF16).ap()

    st = contextlib.ExitStack()
    with st:
        def sb(name, shape, dt):
            return st.enter_context(nc.sbuf_tensor(name, list(shape), dt))

        g.ident_bf = sb("ident_bf_sb", [128, 128], BF16)
        g.ident_f = sb("ident_f_sb", [128, 128], F32)
        g.maskf = sb("maskf_sb", [128, 128], F32)
        g.maskb = sb("maskb_sb", [128, 128], F32)
        g.rmask = sb("rmask_sb", [128, T], BF16)
        g.coresel = sb("coresel_sb", [128, 2, NCORES], F32)
        g.modT = sb("modT", [128, 6 * KC, 2], F32)
        g.lbs = sb("lbs", [128, NH, 2, 4], F32)
        g.normwT = sb("normwT", [128, NH], F32)
        g.poolscT = sb("poolscT", [128, 4], F32)
        g.convwT = sb("convwT", [128, 4, 3], F32)
        g.ffncw = sb("ffncw", [128, NJ, 3], F32)
        g.ffncb = sb("ffncb", [128, NJ], F32)
        g.NXC = NX // 64
        g.lw = sb("lw", [128, NH, 2, g.NXC], F32)
        g.LD = sb("LD", [128, NH, 2], F32)
        g.S_in = sb("S_in", [128, 2, NH, 128], F32)

        ph = Phase(nc, "init")
        b0 = Buf()
        ph.dma("sp", g.ident_bf[:], I["ident_bf"][:, :], writes=[b0])
        ph.dma("sp", g.ident_f[:], I["ident_f"][:, :], writes=[b0])
        ph.dma("sp", g.maskf[:], I["maskf"][:, :], writes=[b0])
        ph.dma("sp", g.maskb[:], I["maskb"][:, :], writes=[b0])
        ph.dma("sp", g.rmask[:], I["rmask"].partition_broadcast(128)[:, 0, :], writes=[b0])
        ph.dma("sp", g.coresel[:], I["coresel"][:, :, :], writes=[b0])
        ph.dma("sp", g.xres[0:NCTX, :], I["ctx"][:, :], writes=[b0])
        ph.dma("sp", g.xres[NCTX:T, :], I["x"][:, :], writes=[b0])
        ph.run()

        if cfg.wshard:
            gather_weights(g)

        for l in cfg.layers:
            if "mod" in cfg.phases:
                phase_mod(g, l)
            if "hc" in cfg.phases or "H" in cfg.phases:
                phase_hconst(g, l)
            with nc.sbuf_tensor(_n("hxT"), [128, KC, T], BF16) as hxT:
                if "ln1" in cfg.phases:
                    phase_ln_mod(g, l, hxT, 0, 1, list(range(NT)), "a")
                if "hxT" in g.dbg and l == 0:
                    dump(g, [(g.dbg["hxT"][:, :, :], hxT[:]), (g.dbg["mod"][:, :], g.mod_d[:, :])])
                if "H" in cfg.phases:
                    phase_H(g, l, hxT)
                if "L" in cfg.phases:
                    phase_L(g, l, hxT)
            if "X" in cfg.phases:
                phase_X(g, l)
            with nc.sbuf_tensor(_n("catT"), [128, KC, T], BF16) as catT:
                if "C" in cfg.phases:
                    phase_C(g, l, catT)
                if "catT" in g.dbg and l == 0:
                    dump(g, [(g.dbg["catT"][:, :, :], catT[:])])
                if "O" in cfg.phases:
                    phase_O1(g, l, catT)
            if "O" in cfg.phases:
                phase_resid_ln(g, l, 1)
                if "x1" in g.dbg and l == 0:
                    dump(g, [(g.dbg["x1"][:, :], g.xres[:, :])])
            with nc.sbuf_tensor(_n("hx2T"), [128, KC, T], BF16) as hx2T:
                if "U" in cfg.phases:
                    phase_ln_mod(g, l, hx2T, 3, 4, tiles_live(cfg, l), "b")
                    phase_U(g, l, hx2T)
            if "D" in cfg.phases:
                phase_D1(g, l)
                phase_resid_ln(g, l, 2)
                if "x2" in g.dbg and l == 0:
                    dump(g, [(g.dbg["x2"][:, :], g.xres[:, :])])

        ph = Phase(nc, "fin")
        ph.dma("sp", out[:, :], g.xres[NCTX:T, :])
        ph.run()
    return nc


def live_ctx(l):
    return l < DEPTH - 1


def tiles_live(cfg, l):
    return list(range(cfg.NT)) if live_ctx(l) else list(range(cfg.NTC, cfg.NT))


def dump(g, pairs):
    ph = Phase(g.nc, _n("dump"))
    for o, i in pairs:
        ph.dma("sp", o, i)
    ph.run()


def gather_weights(g):
    nc = g.nc
    names = [(nm, l) for l in g.cfg.layers for nm in BIGW if nm in g.cfg.need]
    s1 = nc.alloc_semaphore(name="gw_s")
    s2 = nc.alloc_semaphore(name="gw_c")
    with nc.Block() as block:
        @block.gpsimd
        def _(e):
            for i, (nm, l) in enumerate(names):
                r, c = BIGW[nm]
                bounce = nc.dram_tensor(f"bnc_{nm}{l}", [r // NCORES, c], F32).ap()
                e.dma_start(out=bounce[:, :], in_=g.I[f"{nm}{l}"][:, :]).then_inc(s1, 16)
                e.wait_ge(s1, 16 * (i + 1))
                e.collective_compute("AllGather", ALU.bypass, replica_groups=[list(range(NCORES))],
                                     ins=[bounce.opt()], outs=[g.W[(nm, l)].opt()]).then_inc(s2)
                e.wait_ge(s2, i + 1)


def transpose_rows(ph, g, rows_ap, b_rows, R, N, ps_ap, dst, b_dst, eng="dve", b_ps=None):
    b_ps = b_ps or Buf()
    for q in range(N // 128):
        ph.tr(ps_ap[:, q, :], rows_ap[0:R, q * 128:(q + 1) * 128], g.ident_f[0:R, 0:R], [b_rows], [b_ps])
    return ph.copy(eng, dst, ps_ap, [b_ps], [b_dst])


def phase_mod(g, l):
    nc, I = g.nc, g.I
    NB = 6 * D // 512
    with contextlib.ExitStack() as _st:
        cc = _st.enter_context(nc.sbuf_tensor(_n("m_cc"), [2, D], F32))
        sc = _st.enter_context(nc.sbuf_tensor(_n("m_sc"), [2, D], F32))
        scT = _st.enter_context(nc.sbuf_tensor(_n("m_scT"), [128, KC, 2], F32))
        bada = _st.enter_context(nc.sbuf_tensor(_n("m_bada"), [2, 6 * D], F32))
        mod = _st.enter_context(nc.sbuf_tensor(_n("m_mod"), [2, 6 * D], F32))
        w0 = _st.enter_context(nc.sbuf_tensor(_n("m_w0"), [128, KC, 512], F32))
        w1 = _st.enter_context(nc.sbuf_tensor(_n("m_w1"), [128, KC, 512], F32))
        pT = _st.enter_context(nc.psum_tensor(_n("m_pT"), [128, KC, 2], F32))
        pm0 = _st.enter_context(nc.psum_tensor(_n("m_pm0"), [2, 512], F32))
        pm1 = _st.enter_context(nc.psum_tensor(_n("m_pm1"), [2, 512], F32))
        pT2 = _st.enter_context(nc.psum_tensor(_n("m_pT2"), [128, 6 * KC, 2], F32))
        ph = Phase(nc, f"mod{l}")
        b_cc, b_sc, b_scT, b_bada, b_mod, b_modT = (Buf() for _ in range(6))
        b_w = [Buf(), Buf()]
        b_pm = [Buf(), Buf()]
        wt = [w0, w1]
        pm = [pm0, pm1]
        ph.dma("sp", cc[:], I["cc"][:, :], writes=[b_cc])
        ph.dma("sp", bada[0:1, :], g.W[("b_ada", l)][:, :], writes=[b_bada])
        ph.dma("sp", bada[1:2, :], g.W[("b_ada", l)][:, :], writes=[b_bada])
        ph.act(sc[:], cc[:], AF.Silu, [b_cc], [b_sc])
        transpose_rows(ph, g, sc[:], b_sc, 2, D, pT[:], scT[:], b_scT)
        wsrc = g.W[("w_ada", l)].rearrange("(j p) n -> p j n", p=128)
        for nb in range(NB):
            s = nb % 2
            ph.dma("sp", wt[s][:], wsrc[:, :, nb * 512:(nb + 1) * 512], writes=[b_w[s]])
            for j in range(KC):
                ph.mm(pm[s][:], scT[:, j, :], wt[s][:, j, :], j == 0, j == KC - 1, [b_scT, b_w[s]], [b_pm[s]])
            ph.tt("dve", mod[:, nb * 512:(nb + 1) * 512], pm[s][:], bada[:, nb * 512:(nb + 1) * 512], ALU.add,
                  [b_pm[s], b_bada], [b_mod])
        ph.dma("sp", g.mod_d[:, :], mod[:], reads=[b_mod])
        transpose_rows(ph, g, mod[:], b_mod, 2, 6 * D, pT2[:], g.modT[:], b_modT)
        ph.run()


def phase_hconst(g, l):
    nc, I = g.nc, g.I
    with contextlib.ExitStack() as _st:
        rows = _st.enter_context(nc.sbuf_tensor(_n("hc_rows"), [4, D_FF], F32))
        lbT = _st.enter_context(nc.sbuf_tensor(_n("hc_lbT"), [128, NH, 4], F32))
        tmp = _st.enter_context(nc.sbuf_tensor(_n("hc_t"), [128, NH, 2, 1], F32))
        ps = _st.enter_context(nc.psum_tensor(_n("hc_ps"), [128, NJ, 4], F32))
        ph = Phase(nc, f"hc{l}")
        b_rows, b_lbT, b_x, b_ps = Buf(), Buf(), Buf(), Buf()
        ph.dma("sp", rows[0:4, 0:D_HGRN], I["lb_logits"][:, :], writes=[b_rows])
        transpose_rows(ph, g, rows[:, 0:D_HGRN], b_rows, 4, D_HGRN, ps[:, 0:NH, 0:4], lbT[:], b_lbT, b_ps=b_ps)
        if l == 0:
            ph.memset("dve", g.lbs[:, :, :, 0:1], 0.0, [b_x])
        else:
            for d in range(2):
                ph.tt("dve", tmp[:, :, d, :], lbT[:, :, 2 * d + 1:2 * d + 2], lbT[:, :, 2 * d:2 * d + 1], ALU.subtract,
                      [b_lbT], [b_x])
            ph.act(g.lbs[:, :, :, 0:1], tmp[:], AF.Sigmoid, [b_x], [b_x])
        ph.ts("dve", g.lbs[:, :, :, 1:2], g.lbs[:, :, :, 0:1], -1.0, 1.0, ALU.mult, ALU.add, [b_x], [b_x])
        ph.ts("dve", g.lbs[:, :, :, 2:3], g.lbs[:, :, :, 0:1], -1.0, None, ALU.add, None, [b_x], [b_x])
        ph.ts("dve", g.lbs[:, :, :, 3:4], g.lbs[:, :, :, 0:1], -1.0, F_MIN, ALU.mult, ALU.add, [b_x], [b_x])

        def rowsT(name, R, N, dst):
            ph.dma("sp", rows[0:R, 0:N], g.W[(name, l)][:, :], writes=[b_rows])
            transpose_rows(ph, g, rows[:, 0:N], b_rows, R, N, ps[:, 0:N // 128, 0:R], dst, Buf(), b_ps=b_ps)
        rowsT("hgrn_norm_w", 1, D_HGRN, g.normwT[:].unsqueeze(2))
        rowsT("pool_scale", 1, 512, g.poolscT[:].unsqueeze(2))
        rowsT("conv_w", 3, 512, g.convwT[:])
        rowsT("ffn_conv_w", 3, D_FF, g.ffncw[:])
        rowsT("ffn_conv_b", 1, D_FF, g.ffncb[:].unsqueeze(2))
        ph.run()


def ln_stats(ph, x, b_x, stt, b_st, mv, b_mv, rs, b_rs):
    for q in range(4):
        ph.op("dve", "bn_stats", [b_x], [b_st], out=stt[:, q, :], in_=x[:, q * 512:(q + 1) * 512])
    ph.op("dve", "bn_aggr", [b_st], [b_mv], out=mv, in_=stt)
    ph.act(rs[:, 0:1], mv[:, 1:2], AF.Sqrt, [b_mv], [b_rs], bias=EPS, scale=1.0)
    ph.op("dve", "reciprocal", [b_rs], [b_rs], out=rs[:, 0:1], in_=rs[:, 0:1])
    ph.stt(rs[:, 1:2], mv[:, 0:1], -1.0, rs[:, 0:1], ALU.mult, ALU.mult, [b_mv, b_rs], [b_rs])


def phase_ln_mod(g, l, dstT, which_shift, which_scale, tiles, tag):
    nc = g.nc
    NTC = g.cfg.NTC
    modT = g.modT
    with contextlib.ExitStack() as _st:
        x0 = _st.enter_context(nc.sbuf_tensor(_n("p1_x0"), [128, D], F32))
        x1 = _st.enter_context(nc.sbuf_tensor(_n("p1_x1"), [128, D], F32))
        xh0 = _st.enter_context(nc.sbuf_tensor(_n("p1_xh0"), [128, D], BF16))
        xh1 = _st.enter_context(nc.sbuf_tensor(_n("p1_xh1"), [128, D], BF16))
        stt = _st.enter_context(nc.sbuf_tensor(_n("p1_st"), [128, 2, 4, 6], F32))
        mv = _st.enter_context(nc.sbuf_tensor(_n("p1_mv"), [128, 2, 2], F32))
        rs = _st.enter_context(nc.sbuf_tensor(_n("p1_rs"), [128, 2, 2], F32))
        sc1p = _st.enter_context(nc.sbuf_tensor(_n("p1_sc1"), [128, KC, 2], F32))
        pt0 = _st.enter_context(nc.psum_tensor(_n("p1_pt0"), [128, KC, 128], BF16))
        pt1 = _st.enter_context(nc.psum_tensor(_n("p1_pt1"), [128, KC, 128], BF16))
        ph = Phase(nc, f"ln{l}{tag}")
        xs, xhs, pts = [x0, x1], [xh0, xh1], [pt0, pt1]
        b_x, b_xh, b_pt = [Buf(), Buf()], [Buf(), Buf()], [Buf(), Buf()]
        b_st, b_mv, b_rs = [Buf(), Buf()], [Buf(), Buf()], [Buf(), Buf()]
        b_sc = Buf()
        ph.ts("dve", sc1p[:], modT[:, which_scale * KC:(which_scale + 1) * KC, :], 1.0, None, ALU.add, None, [], [b_sc])
        for i, tt in enumerate(tiles):
            s = i % 2
            r = 1 if tt < NTC else 0
            ph.dma("sp", xs[s][:], g.xres[tt * 128:(tt + 1) * 128, :], writes=[b_x[s]])
            ln_stats(ph, xs[s], b_x[s], stt[:, s], b_st[s], mv[:, s], b_mv[s], rs[:, s], b_rs[s])
            ph.act(xhs[s][:], xs[s][:], AF.Identity, [b_x[s], b_rs[s]], [b_xh[s]], bias=rs[:, s, 1:2], scale=rs[:, s, 0:1])
            for j in range(KC):
                ph.tr(pts[s][:, j, :], xhs[s][:, j * 128:(j + 1) * 128], g.ident_bf[:], [b_xh[s]], [b_pt[s]])
            for j in range(KC):
                o = dstT[:, j, tt * 128:(tt + 1) * 128]
                sc_ap = sc1p[:, j, r:r + 1]
                sh_ap = modT[:, which_shift * KC + j, r:r + 1]
                if j % 2 == 0:
                    ph.ts("dve", o, pts[s][:, j, :], sc_ap, sh_ap, ALU.mult, ALU.add, [b_pt[s], b_sc], [])
                else:
                    ph.act(o, pts[s][:, j, :], AF.Identity, [b_pt[s], b_sc], [], bias=sh_ap, scale=sc_ap)
        ph.run()


def phase_H(g, l, hxT):
    nc, cfg = g.nc, g.cfg
    T, NT, NTC, NX, NCTX = cfg.T, cfg.NT, cfg.NTC, cfg.NX, cfg.NCTX
    NCH, NCC, NXC = T // 64, NCTX // 64, NX // 64
    groups = token_groups(cfg)
    Win = g.W[("w_in", l)].rearrange("(j p) n -> p j n", p=128)
    oloc_v = g.oloc_d.rearrange("(tt p) h v -> p tt h v", p=128)
    with contextlib.ExitStack() as _st:
        wblk = _st.enter_context(nc.sbuf_tensor(_n("h_w"), [128, 5, KC, 128], BF16))
        qT = _st.enter_context(nc.sbuf_tensor(_n("h_q"), [128, T], F32))
        sS = _st.enter_context(nc.sbuf_tensor(_n("h_s"), [128, T], F32))
        gl = _st.enter_context(nc.sbuf_tensor(_n("h_gl"), [128, T], F32))
        bb = _st.enter_context(nc.sbuf_tensor(_n("h_bb"), [128, T], F32))
        E = _st.enter_context(nc.sbuf_tensor(_n("h_E"), [128, T], F32))
        QK = _st.enter_context(nc.sbuf_tensor(_n("h_QK"), [128, 2, 2, T], BF16))
        itm = _st.enter_context(nc.sbuf_tensor(_n("h_itm"), [128, NT, 128], BF16))
        oloc = _st.enter_context(nc.sbuf_tensor(_n("h_oloc"), [128, NT, 128], F32))
        seg = _st.enter_context(nc.sbuf_tensor(_n("h_seg"), [128, 2, 2, 512], BF16))
        tab = _st.enter_context(nc.sbuf_tensor(_n("h_tab"), [128, 2, 3, NCH], F32))
        etab = _st.enter_context(nc.sbuf_tensor(_n("h_etab"), [128, 2, 3, NCH], F32))
        tabx = _st.enter_context(nc.sbuf_tensor(_n("h_tabx"), [128, 2, NXC], F32))
        ones = _st.enter_context(nc.sbuf_tensor(_n("h_ones"), [128, NXC], F32))
        S = _st.enter_context(nc.sbuf_tensor(_n("h_S"), [128, 2, 128], F32))
        Sbf = _st.enter_context(nc.sbuf_tensor(_n("h_Sbf"), [128, 2, 128], BF16))
        tmpS = _st.enter_context(nc.sbuf_tensor(_n("h_tS"), [128, 2, 128], F32))
        AT = _st.enter_context(nc.sbuf_tensor(_n("h_AT"), [128, 2, 2, 128], BF16))
        KTs = _st.enter_context(nc.sbuf_tensor(_n("h_KT"), [128, 2, 2, 128], BF16))
        pp0 = _st.enter_context(nc.psum_tensor(_n("h_pp0"), [128, 512], F32))
        pp1 = _st.enter_context(nc.psum_tensor(_n("h_pp1"), [128, 512], F32))
        pof0 = _st.enter_context(nc.psum_tensor(_n("h_pof0"), [128, 512], F32))
        pob0 = _st.enter_context(nc.psum_tensor(_n("h_pob0"), [128, 512], F32))
        asf = _st.enter_context(nc.psum_tensor(_n("h_asf"), [128, 512], F32))
        asb = _st.enter_context(nc.psum_tensor(_n("h_asb"), [128, 512], F32))
        tf_ = _st.enter_context(nc.psum_tensor(_n("h_tf"), [128, 1024], BF16))
        tb_ = _st.enter_context(nc.psum_tensor(_n("h_tb"), [128, 1024], BF16))
        ph = Phase(nc, f"H{l}")
        pp = [pp0, pp1]
        po = [[pof0, pof0], [pob0, pob0]]
        pA = [asf[:, 0:128], asb[:, 0:128]]
        pS = [asf[:, 128:256], asb[:, 128:256]]
        pKT = [tf_[:, 0:128], tb_[:, 0:128]]
        piT = [tf_[:, 128:256], tb_[:, 128:256]]
        masks = [g.maskf, g.maskb]
        b_pp = [Buf(), Buf()]
        _bf, _bb = Buf(), Buf()
        b_po = [[_bf, _bf], [_bb, _bb]]
        b_pA = [Buf(), Buf()]
        b_pS = b_pA
        b_pKT = [Buf(), Buf()]
        b_piT = b_pKT
        b_w = [Buf() for _ in range(5)]
        b_q, b_s, b_gl, b_bb, b_E = Buf(), Buf(), Buf(), Buf(), Buf()
        b_QK = [[Buf(), Buf()], [Buf(), Buf()]]
        b_itm = Buf()
        b_oloc = [Buf() for _ in range(NT)]
        b_seg = [[Buf(), Buf()], [Buf(), Buf()]]
        b_tab, b_etab = [Buf(), Buf()], [Buf(), Buf()]
        b_tabx, b_ones, b_lw = Buf(), Buf(), Buf()
        b_S, b_Sbf, b_tS = [Buf(), Buf()], [Buf(), Buf()], [Buf(), Buf()]
        b_AT = [[Buf(), Buf()], [Buf(), Buf()]]
        b_KT = [[Buf(), Buf()], [Buf(), Buf()]]
        ph.memset("pool", ones[:], 1.0, [b_ones])
        ph.memset("pool", AT[:], 0.0, [b_AT[0][0], b_AT[0][1], b_AT[1][0], b_AT[1][1]])
        cnt = {"pp": 0, "si": 0, "sg": 0, "pi": 0}
        bb3 = bb[:].rearrange("p (c t) -> p c t", t=64)

        def proj(blk, evac):
            for (t0, n, _) in groups:
                s = cnt["pp"] % 2
                cnt["pp"] += 1
                for j in range(KC):
                    ph.mm(pp[s][:, 0:n], wblk[:, blk, j, :], hxT[:, j, t0:t0 + n], j == 0, j == KC - 1,
                          [b_w[blk]], [b_pp[s]])
                evac(t0, n, pp[s][:, 0:n], b_pp[s])

        for h in range(NH):
            cols = [k * D_HGRN + h * 128 for k in range(5)]
            for blk in range(5):
                ph.dma("pool", wblk[:, blk], Win[:, :, cols[blk]:cols[blk] + 128], writes=[b_w[blk]])
            ph.memset("pool", oloc[:], 0.0, b_oloc)

            proj(0, lambda t0, n, p, bp: ph.act(qT[:, t0:t0 + n], p, AF.Silu, [bp], [b_q]))

            def evac_i(t0, n, p, bp):
                s = cnt["si"] % 2
                cnt["si"] += 1
                ph.copy("dve", seg[:, 0, s, 0:n], p, [bp], [b_seg[0][s]])
                for k in range(n // 128):
                    s2 = cnt["pi"] % 2
                    cnt["pi"] += 1
                    tt = t0 // 128 + k
                    ph.tr(piT[s2], seg[:, 0, s, k * 128:(k + 1) * 128], g.ident_bf[:], [b_seg[0][s]], [b_piT[s2]])
                    ph.copy("act", itm[:, tt, :], piT[s2], [b_piT[s2]], [b_itm])
            proj(1, evac_i)

            def evac_g(t0, n, p, bp, h=h):
                s = cnt["sg"] % 2
                cnt["sg"] += 1
                ph.act(seg[:, 1, s, 0:n], p, AF.Silu, [bp], [b_seg[1][s]])
                ph.dma("sp", g.sg_d[:, h, t0:t0 + n], seg[:, 1, s, 0:n], reads=[b_seg[1][s]])
            proj(4, evac_g)

            for d in range(2):
                lb = g.lbs[:, h, d, 0:1]
                oml = g.lbs[:, h, d, 1:2]
                noml = g.lbs[:, h, d, 2:3]
                fml = g.lbs[:, h, d, 3:4]
                proj(2 + d, lambda t0, n, p, bp: ph.act(sS[:, t0:t0 + n], p, AF.Sigmoid, [bp], [b_s]))
                ph.ts("dve", gl[:], sS[:], oml, fml, ALU.mult, ALU.max, [b_s], [b_gl])
                ph.act(gl[:], gl[:], AF.Ln, [b_gl], [b_gl], bias=lb, scale=1.0)
                ph.ts("pool", sS[:], sS[:], noml, oml, ALU.mult, ALU.add, [b_s, b_gl], [b_s])
                ph.op("dve", "tensor_tensor_scan", [b_gl], [b_bb], out=bb[:], data0=g.rmask[:, 0:T], data1=gl[:],
                      initial=0.0, op0=ALU.mult, op1=ALU.add)
                ph.copy("pool", tab[:, d, 0, :], bb3[:, :, 31], [b_bb], [b_tab[d]])
                ph.copy("pool", tab[:, d, 1, :], bb3[:, :, 63], [b_bb], [b_tab[d]])
                ph.tt("pool", tab[:, d, 2, :], tab[:, d, 1, :], tab[:, d, 0, :], ALU.subtract, [b_tab[d]], [b_tab[d]])
                ph.act(etab[:, d], tab[:, d], AF.Exp, [b_tab[d]], [b_etab[d]])
                ph.tt("pool", bb3, bb3, tab[:, d, 0, :].unsqueeze(2).to_broadcast([128, NCH, 64]), ALU.subtract,
                      [b_bb, b_tab[d]], [b_bb])
                if d == 1:
                    ph.tt("pool", bb[:], gl[:], bb[:], ALU.subtract, [b_gl, b_bb], [b_bb])
                if os.environ.get("K_DEBUG_CLAMP"):
                    ph.ts("pool", bb[:], bb[:], 40.0, -40.0, ALU.min, ALU.max, [b_bb], [b_bb])
                ph.act(E[:], bb[:], AF.Exp, [b_bb], [b_E])
                ph.tt("dve", QK[:, d, 0, :], qT[:], E[:], ALU.mult, [b_q, b_E], [b_QK[d][0]])
                ph.act(E[:], bb[:], AF.Exp, [b_bb], [b_E], scale=-1.0)
                ph.tt("dve", QK[:, d, 1, :], sS[:], E[:], ALU.mult, [b_s, b_E], [b_QK[d][1]])
                ph.dma("sp", g.qp_d[:, 2 * h + d, :], QK[:, d, 0, NCTX:T], reads=[b_QK[d][0]])
                totx = tab[:, d, 1, NCC:NCH]
                ph.op("dve", "tensor_tensor_scan", [b_tab[d], b_ones], [b_tabx], out=tabx[:, 0, :], data0=ones[:],
                      data1=totx, initial=0.0, op0=ALU.mult, op1=ALU.add)
                ph.copy("pool", g.LD[:, h, d:d + 1], tabx[:, 0, NXC - 1:NXC], [b_tabx], [b_lw])
                if d == 0:
                    ph.tt("pool", tabx[:, 1, :], tabx[:, 0, :], totx, ALU.subtract, [b_tabx, b_tab[d]], [b_tabx])
                    ph.tt("pool", g.lw[:, h, 0, :], tabx[:, 1, :], tab[:, 0, 0, NCC:NCH], ALU.add,
                          [b_tabx, b_tab[d]], [b_lw])
                else:
                    ph.tt("pool", tabx[:, 1, :], tab[:, 1, 2, NCC:NCH], tabx[:, 0, :], ALU.subtract,
                          [b_tabx, b_tab[d]], [b_tabx])
                    ph.ts("dve", g.lw[:, h, 1, :], tabx[:, 1, :], g.LD[:, h, 1:2], None, ALU.add, None,
                          [b_tabx, b_lw], [b_lw])

            def chain(d, tiles, kind, h=h):
                ph.memset("pool", S[:, d, :], 0.0, [b_S[d]])
                ph.memset("pool", Sbf[:, d, :], 0.0, [b_Sbf[d]])
                order = []
                for tt in tiles:
                    for hf in ((0, 1) if d == 0 else (1, 0)):
                        order.append((tt, hf, 2 * tt + hf))
                r_sp, r_c2 = (0, 2) if d == 0 else (2, 0)
                idx = 0
                for it, tt in enumerate(tiles):
                    sl = slice(tt * 128, (tt + 1) * 128)
                    s = it % 2
                    ph.tr(pKT[d], QK[:, d, 1, sl], g.ident_bf[:], [b_QK[d][1]], [b_pKT[d]])
                    ph.copy("act", KTs[:, d, s, :], pKT[d], [b_pKT[d]], [b_KT[d][s]])
                    ph.mm(pA[d], QK[:, d, 1, sl], QK[:, d, 0, sl], True, True, [b_QK[d][0], b_QK[d][1]], [b_pA[d]])
                    ph.op("dve", "copy_predicated", [b_pA[d]], [b_AT[d][s]], out=AT[:, d, s, :],
                          mask=masks[d][:].bitcast(mybir.dt.uint32), data=pA[d])
                    pot = po[d][s][:, 0:128]
                    ph.mm(pot, AT[:, d, s, :], itm[:, tt, :], True, False, [b_AT[d][s], b_itm], [b_po[d][s]])
                    for k2 in range(2):
                        _, hf, ci = order[idx]
                        nxt = order[idx + 1][2] if idx + 1 < len(order) else None
                        idx += 1
                        hs = slice(hf * 64, hf * 64 + 64)
                        ts_ = slice(tt * 128 + hf * 64, tt * 128 + hf * 64 + 64)
                        ph.mm(po[d][s][hs, 0:128], QK[:, d, 0, ts_], Sbf[:, d, :], False, True,
                              [b_QK[d][0], b_Sbf[d]], [b_po[d][s]])
                        ph.mm(pS[d], KTs[hs, d, s, :], itm[hs, tt, :], True, True, [b_KT[d][s], b_itm], [b_pS[d]])
                        ph.act(tmpS[:, d, :], pS[d], AF.Identity, [b_pS[d], b_etab[d]], [b_tS[d]],
                               scale=etab[:, d, r_c2, ci:ci + 1])
                        ph.stt(S[:, d, :], S[:, d, :], etab[:, d, 1, ci:ci + 1], tmpS[:, d, :], ALU.mult, ALU.add,
                               [b_tS[d], b_etab[d]], [b_S[d]])
                        if nxt is not None:
                            ph.ts("pool", Sbf[:, d, :], S[:, d, :], etab[:, d, r_sp, nxt:nxt + 1], None, ALU.mult, None,
                                  [b_S[d], b_etab[d]], [b_Sbf[d]])
                    ph.tt("dve", oloc[:, tt, :], pot, oloc[:, tt, :], ALU.add, [b_po[d][s]], [b_oloc[tt]])
                    yield
                if kind == "ctx":
                    dst = g.sctx_d[:, d, h, :]
                else:
                    dst = g.xch_src[:, (d * NH + h) * 128:(d * NH + h + 1) * 128]
                ph.dma("sp", dst, S[:, d, :], reads=[b_S[d]])
                yield

            for gens in ([chain(0, list(range(NTC)), "ctx"), chain(1, list(range(NTC - 1, -1, -1)), "ctx")],
                         [chain(0, list(range(NTC, NT)), "x"), chain(1, list(range(NT - 1, NTC - 1, -1)), "x")]):
                for _ in itertools.zip_longest(*gens):
                    pass
            tl = tiles_live(cfg, l)
            ph.dma("sp", oloc_v[:, tl[0]:tl[-1] + 1, h, :], oloc[:, tl[0]:tl[-1] + 1, :], reads=b_oloc)
        ph.dma("sp", g.xch_src[:, 2 * NH * 128:2 * NH * 128 + 2 * NH], g.LD[:].rearrange("p h d -> p (h d)"), reads=[b_lw])
        ph.run()


def phase_L(g, l, hxT):
    nc, cfg = g.nc, g.cfg
    T, NX, NCTX = cfg.T, cfg.NX, cfg.NCTX
    lc = live_ctx(l)
    groups = token_groups(cfg, live_ctx=lc)
    Win = g.W[("w_in", l)].rearrange("(j p) n -> p j n", p=128)
    NXR = NX // 64
    CW = NCTX + 2 * PAD
    with contextlib.ExitStack() as _st:
        wblk = _st.enter_context(nc.sbuf_tensor(_n("l_w"), [128, 3, 2, KC, 128], BF16))
        pw = _st.enter_context(nc.sbuf_tensor(_n("l_pw"), [128, 4, 128], BF16))
        Px = _st.enter_context(nc.sbuf_tensor(_n("l_Px"), [128, 2, NXR, RW], F32))
        Pc = _st.enter_context(nc.sbuf_tensor(_n("l_Pc"), [128, 2, CW], F32))
        rcx = _st.enter_context(nc.sbuf_tensor(_n("l_rcx"), [128, 4, RW], F32))
        rcc = _st.enter_context(nc.sbuf_tensor(_n("l_rcc"), [128, 4, CW], F32))
        vv = _st.enter_context(nc.sbuf_tensor(_n("l_v"), [128, T], F32))
        tmp = _st.enter_context(nc.sbuf_tensor(_n("l_t"), [128, T], F32))
        Bv = _st.enter_context(nc.sbuf_tensor(_n("l_B"), [128, T], F32))
        dbf = _st.enter_context(nc.sbuf_tensor(_n("l_dbf"), [128, T], BF16))
        cat = _st.enter_context(nc.sbuf_tensor(_n("l_cat"), [128, 2, T], BF16))
        pp0 = _st.enter_context(nc.psum_tensor(_n("l_pp0"), [128, 512], F32))
        pp1 = _st.enter_context(nc.psum_tensor(_n("l_pp1"), [128, 512], F32))
        pp2 = _st.enter_context(nc.psum_tensor(_n("l_pp2"), [128, 512], F32))
        pp3 = _st.enter_context(nc.psum_tensor(_n("l_pp3"), [128, 512], F32))
        ph = Phase(nc, f"L{l}")
        pp = [pp0, pp1, pp2, pp3]
        b_pp = [Buf() for _ in range(4)]
        b_w = [[Buf(), Buf()] for _ in range(3)]
        b_pw, b_P, b_Pc, b_rc, b_v, b_t, b_B, b_dbf = (Buf() for _ in range(8))
        b_cat = [Buf(), Buf()]
        cnt = {"pp": 0, "w": 0, "cat": 0}
        t0x = NCTX
        def xin(buf, a, b_):
            return Px[:, buf, :, a:b_]

        def cin(buf, a, b_):
            return Pc[:, buf, a:b_]

        def xflat(ap_T):
            return ap_T[:, t0x:T].rearrange("p (r t) -> p r t", t=64)

        ph.dma("pool", pw[:], g.W[("pool_w", l)].rearrange("(g p) q -> p g q", p=128), writes=[b_pw])
        ph.memset("pool", Px[:], 0.0, [b_P])
        ph.memset("pool", Pc[:], 0.0, [b_Pc])
        ph.memset("pool", rcx[:], 0.0, [b_rc])
        ph.memset("pool", rcc[:], 0.0, [b_rc])

        def window_sums(steps, is_ctx, b_buf):
            cur = 0
            a = 1
            W_ = CW if is_ctx else RW
            L = W_
            for _ in range(steps):
                L2 = L - a
                if is_ctx:
                    ph.tt("pool", Pc[:, 1 - cur, 0:L2], Pc[:, cur, 0:L2], Pc[:, cur, a:a + L2], ALU.add, [b_buf], [b_buf])
                else:
                    ph.tt("pool", Px[:, 1 - cur, :, 0:L2], Px[:, cur, :, 0:L2], Px[:, cur, :, a:a + L2], ALU.add,
                          [b_buf], [b_buf])
                cur = 1 - cur
                a *= 2
                L = L2
            return cur

        for gi in range(4):
            w = 2 ** (gi + 1)
            ph.memset("pool", Px[:, 0, 0:1, PAD:PAD + 64], 1.0, [b_P])
            cur = window_sums(gi + 1, False, b_P)
            ph.op("dve", "reciprocal", [b_P], [b_rc], out=rcx[:, gi, 0:64], in_=Px[:, cur, 0, PAD - w // 2:PAD - w // 2 + 64])
            ph.memset("pool", Px[:, :, 0:1, :], 0.0, [b_P])
            if lc:
                ph.memset("pool", Pc[:, 0, PAD:PAD + NCTX], 1.0, [b_Pc])
                cur = window_sums(gi + 1, True, b_Pc)
                ph.op("dve", "reciprocal", [b_Pc], [b_rc], out=rcc[:, gi, 0:NCTX],
                      in_=Pc[:, cur, PAD - w // 2:PAD - w // 2 + NCTX])
                ph.memset("pool", Pc[:], 0.0, [b_Pc])

        def load_w(slot, c0):
            s = cnt["w"] % 2
            ph.dma("pool", wblk[:, slot, s], Win[:, :, c0:c0 + 128], writes=[b_w[slot][s]])
            return s

        def proj(slot, s, evac):
            for (t0, n, isc) in groups:
                k = cnt["pp"] % 4
                cnt["pp"] += 1
                for j in range(KC):
                    ph.mm(pp[k][:, 0:n], wblk[:, slot, s, j, :], hxT[:, j, t0:t0 + n], j == 0, j == KC - 1,
                          [b_w[slot][s]], [b_pp[k]])
                evac(t0, n, isc, pp[k][:, 0:n], b_pp[k])

        def pad_dst(buf, t0, n, isc):
            if isc:
                return Pc[:, buf, PAD + t0:PAD + t0 + n], b_Pc
            r0 = (t0 - t0x) // 64
            return Px[:, buf, r0:r0 + n // 64, PAD:PAD + 64], b_P

        def rows(ap2d, n):
            return ap2d.rearrange("p (r t) -> p r t", t=64)

        for gi in range(4):
            w = 2 ** (gi + 1)
            s = load_w(0, 5 * D_HGRN + gi * 128)
            cnt["w"] += 1

            def evac_v(t0, n, isc, p, bp):
                dst, bd = pad_dst(0, t0, n, isc)
                ph.copy("act", dst, p if isc else rows(p, n), [bp], [bd])
                ph.copy("dve", vv[:, t0:t0 + n], p, [bp], [b_v])
            proj(0, s, evac_v)
            cur = window_sums(gi + 1, False, b_P)
            off = PAD - w // 2
            ph.tt("dve", xflat(tmp), Px[:, cur, :, off:off + 64], rcx[:, gi, 0:64].unsqueeze(1).to_broadcast([128, NXR, 64]),
                  ALU.mult, [b_P, b_rc], [b_t])
            if lc:
                cur = window_sums(gi + 1, True, b_Pc)
                ph.tt("dve", tmp[:, 0:NCTX], Pc[:, cur, off:off + NCTX], rcc[:, gi, 0:NCTX], ALU.mult, [b_Pc, b_rc], [b_t])
            lo = 0 if lc else NCTX
            ph.tt("pool", dbf[:, lo:T], tmp[:, lo:T], vv[:, lo:T], ALU.subtract, [b_t, b_v], [b_dbf])
            ph.memset("pool", Px[:, 0, :, 0:PAD], 0.0, [b_P])
            ph.memset("pool", Px[:, 0, :, PAD + 64:RW], 0.0, [b_P])
            if lc:
                ph.memset("pool", Pc[:, 0, 0:PAD], 0.0, [b_Pc])
                ph.memset("pool", Pc[:, 0, PAD + NCTX:CW], 0.0, [b_Pc])
            cs = cnt["cat"] % 2
            cnt["cat"] += 1
            for (t0, n, isc) in groups:
                k = cnt["pp"] % 4
                cnt["pp"] += 1
                ph.mm(pp[k][:, 0:n], pw[:, gi, :], dbf[:, t0:t0 + n], True, True, [b_pw, b_dbf], [b_pp[k]])
                ph.act(cat[:, cs, t0:t0 + n], pp[k][:, 0:n], AF.Identity, [b_pp[k]], [b_cat[cs]], scale=g.poolscT[:, gi:gi + 1])
            ph.dma("sp", g.catl_d[:, gi, lo:T], cat[:, cs, lo:T], reads=[b_cat[cs]])

        for j4 in range(4):
            sB = load_w(0, 5 * D_HGRN + 512 + j4 * 128)
            sC = load_w(1, 5 * D_HGRN + 1024 + j4 * 128)
            sH = load_w(2, 5 * D_HGRN + 1536 + j4 * 128)
            cnt["w"] += 1
            proj(0, sB, lambda t0, n, isc, p, bp: ph.copy("act", Bv[:, t0:t0 + n], p, [bp], [b_B]))
            proj(1, sC, lambda t0, n, isc, p, bp: ph.copy("act", tmp[:, t0:t0 + n], p, [bp], [b_t]))

            def evac_h(t0, n, isc, p, bp):
                dst, bd = pad_dst(0, t0, n, isc)
                if isc:
                    ph.tt("dve", dst, p, tmp[:, t0:t0 + n], ALU.mult, [bp, b_t], [bd])
                else:
                    ph.tt("dve", dst, rows(p, n), rows(tmp[:, t0:t0 + n], n), ALU.mult, [bp, b_t], [bd])
            proj(2, sH, evac_h)
            w0_, w1_, w2_ = (g.convwT[:, j4, k:k + 1] for k in range(3))
            lo = 0 if lc else NCTX
            ph.ts("dve", xflat(vv), Px[:, 0, :, PAD - 1:PAD + 63], w0_, None, ALU.mult, None, [b_P], [b_v])
            ph.stt(xflat(vv), Px[:, 0, :, PAD:PAD + 64], w1_, xflat(vv), ALU.mult, ALU.add, [b_P], [b_v])
            ph.stt(xflat(vv), Px[:, 0, :, PAD + 1:PAD + 65], w2_, xflat(vv), ALU.mult, ALU.add, [b_P], [b_v])
            if lc:
                ph.ts("dve", vv[:, 0:NCTX], Pc[:, 0, PAD - 1:PAD - 1 + NCTX], w0_, None, ALU.mult, None, [b_Pc], [b_v])
                ph.stt(vv[:, 0:NCTX], Pc[:, 0, PAD:PAD + NCTX], w1_, vv[:, 0:NCTX], ALU.mult, ALU.add, [b_Pc], [b_v])
                ph.stt(vv[:, 0:NCTX], Pc[:, 0, PAD + 1:PAD + 1 + NCTX], w2_, vv[:, 0:NCTX], ALU.mult, ALU.add, [b_Pc], [b_v])
            cs = cnt["cat"] % 2
            cnt["cat"] += 1
            ph.tt("pool", cat[:, cs, lo:T], Bv[:, lo:T], vv[:, lo:T], ALU.mult, [b_B, b_v], [b_cat[cs]])
            ph.dma("sp", g.catl_d[:, 4 + j4, lo:T], cat[:, cs, lo:T], reads=[b_cat[cs]])
        ph.run()


def phase_X(g, l):
    nc = g.nc
    XW = g.XW
    with contextlib.ExitStack() as _st:
        G = _st.enter_context(nc.sbuf_tensor(_n("x_G"), [128, NCORES, XW], F32))
        a = _st.enter_context(nc.sbuf_tensor(_n("x_a"), [128, NH], F32))
        ph = Phase(nc, f"X{l}")
        b_src, b_dst, b_G, b_S, b_a = Buf(), Buf(), Buf(), Buf(), Buf()
        ph.op("pool", "collective_compute", [b_src], [b_dst], kind="AllGather", op=ALU.bypass,
              replica_groups=[list(range(NCORES))], ins=[g.xch_src.opt()], outs=[g.xch_dst.opt()])
        ph.dma("sp", G[:], g.xch_dst.rearrange("(r p) c -> p r c", p=128), reads=[b_dst], writes=[b_G])
        ph.dma("sp", g.S_in[:], g.sctx_d[:, :, :, :], writes=[b_S])
        for d in range(2):
            order = range(NCORES) if d == 0 else range(NCORES - 1, -1, -1)
            for i in order:
                sel = g.coresel[:, d, i:i + 1]
                ld = G[:, i, 2 * NH * 128:2 * NH * 128 + 2 * NH].rearrange("p (h d) -> p h d", d=2)[:, :, d]
                ph.act(a[:], ld, AF.Exp, [b_G], [b_a], scale=sel)
                ph.tt("dve", g.S_in[:, d], g.S_in[:, d], a[:].unsqueeze(2).to_broadcast([128, NH, 128]), ALU.mult,
                      [b_a], [b_S])
                ph.stt(g.S_in[:, d], G[:, i, d * NH * 128:(d + 1) * NH * 128].rearrange("p (h v) -> p h v", v=128), sel,
                       g.S_in[:, d], ALU.mult, ALU.add, [b_G], [b_S])
        ph.run()


def phase_C(g, l, catT):
    nc, cfg = g.nc, g.cfg
    T, NT, NTC, NX, NCTX, NXC = cfg.T, cfg.NT, cfg.NTC, cfg.NX, cfg.NCTX, g.NXC
    lc = live_ctx(l)
    lo = 0 if lc else NCTX
    with contextlib.ExitStack() as _st:
        ew = _st.enter_context(nc.sbuf_tensor(_n("c_ew"), [128, NH, 2, NXC], F32))
        qb = _st.enter_context(nc.sbuf_tensor(_n("c_q"), [128, 2, 2 * NH, 512], BF16))
        sgb = _st.enter_context(nc.sbuf_tensor(_n("c_sg"), [128, 2, NH, 512], BF16))
        ol = _st.enter_context(nc.sbuf_tensor(_n("c_ol"), [128, 2, NH, 128], F32))
        Sc = _st.enter_context(nc.sbuf_tensor(_n("c_Sc"), [128, 2, 4, NH, 128], BF16))
        ob = _st.enter_context(nc.sbuf_tensor(_n("c_o"), [128, NH, 128], F32))
        sq = _st.enter_context(nc.sbuf_tensor(_n("c_sq"), [128, NH, 128], F32))
        ss = _st.enter_context(nc.sbuf_tensor(_n("c_ss"), [128, 2, NH], F32))
        on = _st.enter_context(nc.sbuf_tensor(_n("c_on"), [128, 2, NH, 128], BF16))
        nw = _st.enter_context(nc.sbuf_tensor(_n("c_nw"), [128, 2, NH, 128], F32))
        po0 = _st.enter_context(nc.psum_tensor(_n("c_po0"), [128, 4, 128], F32))
        po1 = _st.enter_context(nc.psum_tensor(_n("c_po1"), [128, 4, 128], F32))
        pT0 = _st.enter_context(nc.psum_tensor(_n("c_pT0"), [128, NH, 128], BF16))
        pT1 = _st.enter_context(nc.psum_tensor(_n("c_pT1"), [128, NH, 128], BF16))
        ph = Phase(nc, f"C{l}")
        pT = [pT0, pT1]
        b_ew, b_o, b_sq = Buf(), Buf(), Buf()
        b_q, b_sg, b_ol, b_Sc = [Buf(), Buf()], [Buf(), Buf()], [Buf(), Buf()], [Buf(), Buf()]
        b_po = [Buf(), Buf()]
        b_pT, b_ss, b_on, b_nw = [Buf(), Buf()], [Buf(), Buf()], [Buf(), Buf()], [Buf(), Buf()]
        b_cat = Buf()
        ph.dma("sp", catT[:, 8:16, lo:T], g.catl_d[:, :, lo:T], writes=[b_cat])
        ph.act(ew[:], g.lw[:], AF.Exp, [], [b_ew])
        tiles = tiles_live(cfg, l)
        oloc_v = g.oloc_d
        ngrp = -1
        for it, tt in enumerate(tiles):
            s = it % 2
            isx = tt >= NTC
            t0 = tt * 128
            if isx:
                xg = (t0 - NCTX) // 512
                key = ("x", xg)
                g0 = NCTX + xg * 512
                gn = min(512, T - g0)
            else:
                cg = t0 // 512
                key = ("c", cg)
                g0 = cg * 512
                gn = min(512, NCTX - g0)
            if it == 0 or key != cur_key:
                cur_key = key
                ngrp += 1
                gs = ngrp % 2
                ph.dma("sp", sgb[:, gs, :, 0:gn], g.sg_d[:, :, g0:g0 + gn], writes=[b_sg[gs]])
                if isx:
                    ph.dma("sp", qb[:, gs, :, 0:gn], g.qp_d[:, :, g0 - NCTX:g0 - NCTX + gn], writes=[b_q[gs]])
            tg = t0 - g0
            ph.dma("sp", ol[:, s], oloc_v[t0:t0 + 128, :, :], writes=[b_ol[s]])
            if isx:
                cx0 = (t0 - NCTX) // 64
                for d in range(2):
                    for hf in range(2):
                        eng = "dve" if (d + hf) % 2 == 0 else "pool"
                        ph.tt(eng, Sc[:, s, d * 2 + hf], g.S_in[:, d],
                              ew[:, :, d, cx0 + hf].unsqueeze(2).to_broadcast([128, NH, 128]), ALU.mult,
                              [b_ew], [b_Sc[s]])
                for h in range(NH):
                    pob = po0 if h < 4 else po1
                    bpo = b_po[0] if h < 4 else b_po[1]
                    for hf in range(2):
                        for d in range(2):
                            ph.mm(pob[hf * 64:hf * 64 + 64, h % 4, :],
                                  qb[:, gs, 2 * h + d, tg + hf * 64:tg + hf * 64 + 64], Sc[:, s, d * 2 + hf, h, :],
                                  d == 0, d == 1, [b_q[gs], b_Sc[s]], [bpo])
                ph.tt("dve", ob[:, 0:4], po0[:], ol[:, s, 0:4], ALU.add, [b_po[0], b_ol[s]], [b_o])
                ph.tt("dve", ob[:, 4:8], po1[:], ol[:, s, 4:8], ALU.add, [b_po[1], b_ol[s]], [b_o])
                osrc, bsrc = ob[:], b_o
            else:
                osrc, bsrc = ol[:, s], b_ol[s]
            ph.tt("pool", sq[:], osrc, osrc, ALU.mult, [bsrc], [b_sq])
            ph.op("dve", "tensor_reduce", [b_sq], [b_ss[s]], out=ss[:, s, :], in_=sq[:], axis=AX.X, op=ALU.add)
            ph.act(ss[:, s, :], ss[:, s, :], AF.Sqrt, [b_ss[s]], [b_ss[s]], bias=EPS, scale=1.0 / 128.0)
            ph.op("dve", "reciprocal", [b_ss[s]], [b_ss[s]], out=ss[:, s, :], in_=ss[:, s, :])
            ph.tt("dve", on[:, s], osrc, ss[:, s, :].unsqueeze(2).to_broadcast([128, NH, 128]), ALU.mult,
                  [bsrc, b_ss[s]], [b_on[s]])
            for h in range(NH):
                ph.tr(pT[s][:, h, :], on[:, s, h, :], g.ident_bf[:], [b_on[s]], [b_pT[s]])
            ph.tt("pool", nw[:, s], sgb[:, gs, :, tg:tg + 128], g.normwT[:].unsqueeze(2).to_broadcast([128, NH, 128]),
                  ALU.mult, [b_sg[gs]], [b_nw[s]])
            ph.tt("dve", catT[:, 0:8, t0:t0 + 128], pT[s][:], nw[:, s], ALU.mult, [b_pT[s], b_nw[s]], [b_cat])
        ph.run()


def phase_O1(g, l, catT):
    nc, cfg = g.nc, g.cfg
    T, NTC = cfg.T, cfg.NTC
    Wv = g.W[("w_out", l)].rearrange("(j p) n -> p j n", p=128)
    tiles = tiles_live(cfg, l)
    with contextlib.ExitStack() as _st:
        wsb = _st.enter_context(nc.sbuf_tensor(_n("o_w"), [128, 2, KC, 512], BF16))
        gb = _st.enter_context(nc.sbuf_tensor(_n("o_g"), [128, 2, 2, 512], F32))
        yb = _st.enter_context(nc.sbuf_tensor(_n("o_y"), [128, 4, 512], F32))
        p0 = _st.enter_context(nc.psum_tensor(_n("o_p0"), [128, 512], F32))
        p1 = _st.enter_context(nc.psum_tensor(_n("o_p1"), [128, 512], F32))
        p2 = _st.enter_context(nc.psum_tensor(_n("o_p2"), [128, 512], F32))
        p3 = _st.enter_context(nc.psum_tensor(_n("o_p3"), [128, 512], F32))
        ph = Phase(nc, f"O{l}")
        pp = [p0, p1, p2, p3]
        b_pp = [Buf() for _ in range(4)]
        b_w, b_g = [Buf(), Buf()], [Buf(), Buf()]
        b_y = [Buf() for _ in range(4)]
        n = 0
        for cb in range(4):
            s = cb % 2
            ph.dma("pool", wsb[:, s], Wv[:, :, cb * 512:(cb + 1) * 512], writes=[b_w[s]])
            for r in range(2):
                ph.dma("sp", gb[:, s, r, :], g.mod_d[r:r + 1, 2 * D + cb * 512:2 * D + (cb + 1) * 512].partition_broadcast(128)[:, 0, :],
                       writes=[b_g[s]])
            for tt in tiles:
                k = n % 4
                n += 1
                r = 1 if tt < NTC else 0
                for j in range(KC):
                    ph.mm(pp[k][:], catT[:, j, tt * 128:(tt + 1) * 128], wsb[:, s, j, :], j == 0, j == KC - 1,
                          [b_w[s]], [b_pp[k]])
                ph.tt("dve", yb[:, k, :], pp[k][:], gb[:, s, r, :], ALU.mult, [b_pp[k], b_g[s]], [b_y[k]])
                ph.dma("sp", g.y_d[tt * 128:(tt + 1) * 128, cb * 512:(cb + 1) * 512], yb[:, k, :], reads=[b_y[k]])
        ph.run()


def phase_resid_ln(g, l, which):
    nc, cfg = g.nc, g.cfg
    tiles = tiles_live(cfg, l)
    gname, bname = ("ln1_g", "ln1_b") if which == 1 else ("ln2_g", "ln2_b")
    with contextlib.ExitStack() as _st:
        lg = _st.enter_context(nc.sbuf_tensor(_n("r_g"), [128, D], F32))
        lb = _st.enter_context(nc.sbuf_tensor(_n("r_b"), [128, D], F32))
        xb = _st.enter_context(nc.sbuf_tensor(_n("r_x"), [128, 2, D], F32))
        yb = _st.enter_context(nc.sbuf_tensor(_n("r_y"), [128, 2, D], F32))
        stt = _st.enter_context(nc.sbuf_tensor(_n("r_st"), [128, 2, 4, 6], F32))
        mv = _st.enter_context(nc.sbuf_tensor(_n("r_mv"), [128, 2, 2], F32))
        rs = _st.enter_context(nc.sbuf_tensor(_n("r_rs"), [128, 2, 2], F32))
        ph = Phase(nc, f"R{l}{which}")
        b_c = Buf()
        b_x, b_y, b_st, b_mv, b_rs = ([Buf(), Buf()] for _ in range(5))
        ph.dma("sp", lg[:], g.W[(gname, l)].partition_broadcast(128)[:, 0, :], writes=[b_c])
        ph.dma("sp", lb[:], g.W[(bname, l)].partition_broadcast(128)[:, 0, :], writes=[b_c])
        for i, tt in enumerate(tiles):
            s = i % 2
            rows = slice(tt * 128, (tt + 1) * 128)
            ph.dma("sp", xb[:, s], g.xres[rows, :], writes=[b_x[s]])
            ph.dma("sp", yb[:, s], g.y_d[rows, :], writes=[b_y[s]])
            ph.stt(yb[:, s], xb[:, s], ALPHA, yb[:, s], ALU.mult, ALU.add, [b_x[s]], [b_y[s]])
            ln_stats(ph, yb[:, s], b_y[s], stt[:, s], b_st[s], mv[:, s], b_mv[s], rs[:, s], b_rs[s])
            ph.act(xb[:, s], yb[:, s], AF.Identity, [b_y[s], b_rs[s]], [b_x[s]], bias=rs[:, s, 1:2], scale=rs[:, s, 0:1])
            ph.tt("pool", xb[:, s], xb[:, s], lg[:], ALU.mult, [b_c], [b_x[s]])
            ph.tt("dve", xb[:, s], xb[:, s], lb[:], ALU.add, [b_c], [b_x[s]])
            ph.dma("sp", g.xres[rows, :], xb[:, s], reads=[b_x[s]])
        ph.run()


def phase_U(g, l, hx2T):
    nc, cfg = g.nc, g.cfg
    T, NX, NCTX = cfg.T, cfg.NX, cfg.NCTX
    lc = live_ctx(l)
    groups = token_groups(cfg, live_ctx=lc)
    Wv = g.W[("w_up", l)].rearrange("(j p) n -> p j n", p=128)
    CW = NCTX + 2 * PAD
    with contextlib.ExitStack() as _st:
        wsb = _st.enter_context(nc.sbuf_tensor(_n("u_w"), [128, 2, 2, KC, 128], BF16))
        Px = _st.enter_context(nc.sbuf_tensor(_n("u_Px"), [128, 2, 8, RW], F32))
        Pc = _st.enter_context(nc.sbuf_tensor(_n("u_Pc"), [128, 2, CW], F32))
        acc = _st.enter_context(nc.sbuf_tensor(_n("u_acc"), [128, 2, 512], F32))
        hid = _st.enter_context(nc.sbuf_tensor(_n("u_hid"), [128, 2, T], BF16))
        pu0 = _st.enter_context(nc.psum_tensor(_n("u_pu0"), [128, 512], F32))
        pu1 = _st.enter_context(nc.psum_tensor(_n("u_pu1"), [128, 512], F32))
        pg0 = _st.enter_context(nc.psum_tensor(_n("u_pg0"), [128, 512], F32))
        pg1 = _st.enter_context(nc.psum_tensor(_n("u_pg1"), [128, 512], F32))
        ph = Phase(nc, f"U{l}")
        pu, pg = [pu0, pu1], [pg0, pg1]
        b_pu, b_pg, b_w, b_P, b_Pc, b_acc, b_hid = ([Buf(), Buf()] for _ in range(7))
        ph.memset("pool", Px[:], 0.0, b_P)
        ph.memset("pool", Pc[:], 0.0, b_Pc)
        n = 0
        for j in range(NJ):
            ws = j % 2
            ph.dma("pool", wsb[:, ws, 0], Wv[:, :, j * 128:(j + 1) * 128], writes=[b_w[ws]])
            ph.dma("pool", wsb[:, ws, 1], Wv[:, :, D_FF + j * 128:D_FF + (j + 1) * 128], writes=[b_w[ws]])
            w0_, w1_, w2_ = (g.ffncw[:, j, k:k + 1] for k in range(3))
            bias = g.ffncb[:, j:j + 1]
            for (t0, nt, isc) in groups:
                s = n % 2
                n += 1
                for k in range(KC):
                    ph.mm(pu[s][:, 0:nt], wsb[:, ws, 0, k, :], hx2T[:, k, t0:t0 + nt], k == 0, k == KC - 1, [b_w[ws]], [b_pu[s]])
                for k in range(KC):
                    ph.mm(pg[s][:, 0:nt], wsb[:, ws, 1, k, :], hx2T[:, k, t0:t0 + nt], k == 0, k == KC - 1, [b_w[ws]], [b_pg[s]])
                if isc:
                    ph.copy("act", Pc[:, s, PAD:PAD + nt], pg[s][:, 0:nt], [b_pg[s]], [b_Pc[s]])
                    a_ = acc[:, s, 0:nt]
                    ph.ts("dve", a_, Pc[:, s, PAD - 1:PAD - 1 + nt], w0_, bias, ALU.mult, ALU.add, [b_Pc[s]], [b_acc[s]])
                    ph.stt(a_, Pc[:, s, PAD:PAD + nt], w1_, a_, ALU.mult, ALU.add, [b_Pc[s]], [b_acc[s]])
                    ph.stt(a_, Pc[:, s, PAD + 1:PAD + 1 + nt], w2_, a_, ALU.mult, ALU.add, [b_Pc[s]], [b_acc[s]])
                else:
                    nr = nt // 64
                    ph.copy("act", Px[:, s, 0:nr, PAD:PAD + 64], pg[s][:, 0:nt].rearrange("p (r t) -> p r t", t=64),
                            [b_pg[s]], [b_P[s]])
                    a_ = acc[:, s, 0:nt].rearrange("p (r t) -> p r t", t=64)
                    ph.ts("dve", a_, Px[:, s, 0:nr, PAD - 1:PAD + 63], w0_, bias, ALU.mult, ALU.add, [b_P[s]], [b_acc[s]])
                    ph.stt(a_, Px[:, s, 0:nr, PAD:PAD + 64], w1_, a_, ALU.mult, ALU.add, [b_P[s]], [b_acc[s]])
                    ph.stt(a_, Px[:, s, 0:nr, PAD + 1:PAD + 65], w2_, a_, ALU.mult, ALU.add, [b_P[s]], [b_acc[s]])
                ph.act(acc[:, s, 0:nt], acc[:, s, 0:nt], AF.Silu, [b_acc[s]], [b_acc[s]])
                ph.tt("dve", hid[:, ws, t0:t0 + nt], pu[s][:, 0:nt], acc[:, s, 0:nt], ALU.mult, [b_pu[s], b_acc[s]], [b_hid[ws]])
            lo = 0 if lc else NCTX
            ph.dma("sp", g.hid_d[j, :, lo:T], hid[:, ws, lo:T], reads=[b_hid[ws]])
        ph.run()


def phase_D1(g, l):
    nc, cfg = g.nc, g.cfg
    T, NTC = cfg.T, cfg.NTC
    lc = live_ctx(l)
    groups = token_groups(cfg, live_ctx=lc)
    Wv = g.W[("w_down", l)].rearrange("(j p) n -> p j n", p=128)
    hv = g.hid_d.rearrange("j p t -> p j t")
    with contextlib.ExitStack() as _st:
        hsb = _st.enter_context(nc.sbuf_tensor(_n("d_h"), [128, NJ, 512], BF16))
        wsb = _st.enter_context(nc.sbuf_tensor(_n("d_w"), [128, 2, NJ, 256], BF16))
        gb = _st.enter_context(nc.sbuf_tensor(_n("d_g"), [128, 2, D], F32))
        yb = _st.enter_context(nc.sbuf_tensor(_n("d_y"), [128, 4, 256], F32))
        p0 = _st.enter_context(nc.psum_tensor(_n("d_p0"), [128, 512], F32))
        p1 = _st.enter_context(nc.psum_tensor(_n("d_p1"), [128, 512], F32))
        p2 = _st.enter_context(nc.psum_tensor(_n("d_p2"), [128, 512], F32))
        p3 = _st.enter_context(nc.psum_tensor(_n("d_p3"), [128, 512], F32))
        ph = Phase(nc, f"D{l}")
        pp = [p0, p1, p2, p3]
        b_pp = [Buf() for _ in range(4)]
        b_h, b_g = Buf(), Buf()
        b_w = [Buf(), Buf()]
        b_y = [Buf() for _ in range(4)]
        for r in range(2):
            ph.dma("sp", gb[:, r, :], g.mod_d[r:r + 1, 5 * D:6 * D].partition_broadcast(128)[:, 0, :], writes=[b_g])
        n = 0
        nw = 0
        for (t0, nt, isc) in groups:
            ph.dma("sp", hsb[:, :, 0:nt], hv[:, :, t0:t0 + nt], writes=[b_h])
            r = 1 if isc else 0
            for cq in range(8):
                s = nw % 2
                nw += 1
                ph.dma("pool", wsb[:, s], Wv[:, :, cq * 256:(cq + 1) * 256], writes=[b_w[s]])
                for k in range(nt // 128):
                    kk = n % 4
                    n += 1
                    for j in range(NJ):
                        ph.mm(pp[kk][:, 0:256], hsb[:, j, k * 128:(k + 1) * 128], wsb[:, s, j, :], j == 0, j == NJ - 1,
                              [b_h, b_w[s]], [b_pp[kk]])
                    ph.tt("dve", yb[:, kk, :], pp[kk][:, 0:256], gb[:, r, cq * 256:(cq + 1) * 256], ALU.mult,
                          [b_pp[kk], b_g], [b_y[kk]])
                    ph.dma("sp", g.y_d[t0 + k * 128:t0 + (k + 1) * 128, cq * 256:(cq + 1) * 256], yb[:, kk, :], reads=[b_y[kk]])
        ph.run()


def make_consts(cfg, core):
    ident = np.eye(128, dtype=np.float32)
    s_ = np.arange(128)[:, None]
    t_ = np.arange(128)[None, :]
    same = (s_ // 64) == (t_ // 64)
    maskf = (same & (s_ <= t_)).astype(np.float32)
    maskb = (same & (s_ >= t_)).astype(np.float32)
    rmask = np.ones((1, cfg.T), np.float32)
    rmask[0, ::64] = 0.0
    cs = np.zeros((128, 2, NCORES), np.float32)
    for i in range(NCORES):
        cs[:, 0, i] = 1.0 if i < core else 0.0
        cs[:, 1, i] = 1.0 if i > core else 0.0
    return {"ident_bf": ident.astype(ml_dtypes.bfloat16), "ident_f": ident, "maskf": maskf, "maskb": maskb,
            "rmask": rmask.astype(ml_dtypes.bfloat16), "coresel": cs}


def make_in_maps(cfg, inputs):
    f = lambda a: np.ascontiguousarray(np.asarray(a, np.float32))
    x = f(inputs["x"]).reshape(-1, D)
    ctx = f(inputs["ctx"]).reshape(-1, D)
    cc = np.stack([f(inputs["c"]).reshape(D), f(inputs["c_ctx"]).reshape(D)], 0)
    lbl = f(inputs["lb_logits"]).reshape(2 * DEPTH, D_HGRN)
    maps = []
    for core in range(NCORES):
        m = {"x": np.ascontiguousarray(x[core * cfg.NX:(core + 1) * cfg.NX]), "ctx": ctx, "cc": cc, "lb_logits": lbl}
        for l in cfg.layers:
            for nm, (r, c) in BIGW.items():
                if nm in cfg.need:
                    w = f(inputs[nm][l]).reshape(r, c)
                    if cfg.wshard:
                        w = np.ascontiguousarray(w[core * (r // NCORES):(core + 1) * (r // NCORES)])
                    m[f"{nm}{l}"] = w
            for nm, (r, c) in SMALLW.items():
                m[f"{nm}{l}"] = f(inputs[nm][l]).reshape(r, c)
        m.update(make_consts(cfg, core))
        maps.append(m)
    return maps


def kernel(**inputs):
    cfg = Cfg(wshard=True)
    nc = build_program(cfg)
    maps = make_in_maps(cfg, inputs)
    res = run_bass_kernel_spmd(nc, maps, core_ids=list(range(NCORES)))
    outs = [r["out"] for r in res.results]
    return np.concatenate(outs, 0).reshape(1, NCORES * cfg.NX, D).astype(np.float32)
```
